# Optimizing a Trainium2 kernel written in Bass

```python
import jax, jax.numpy as jnp
from jax import lax
import numpy as np

D_MODEL = 2048
BATCH = 2
SEQ = 4096
DEPTH = 4
DEC_BATCH = 4
DEC_SEQ = 8192
PAST_LEN = 128

GRID_W = 64
N_MEM = 256
N_BRANCH = 3
D_CONV = 1024
CONV_WIDTH = 31
D_SGU = 1024
SGU_CHUNK = 128
SGU_GROUPS = 8
SGU_GROUP_DIM = D_SGU // SGU_GROUPS
NA_HEADS = 16
NA_HEAD_DIM = 64
D_NA = NA_HEADS * NA_HEAD_DIM
NA_ROWS_MAX = 8
NA_COLS = 16
XA_HEADS = 4
XA_HEAD_DIM = D_MODEL // XA_HEADS
D_FF = 4 * D_MODEL
SPLITS = [D_CONV, D_CONV, D_SGU, D_SGU, D_NA, D_NA, D_NA]
D_IN = sum(SPLITS) + N_BRANCH * D_MODEL
EPS = 1e-6

kernel_name = "hybrid_conv_sgu_natten_encoder"


def rms_norm(x, g):
    xf = x.astype(jnp.float32)
    y = xf * lax.rsqrt(jnp.mean(xf * xf, axis=-1, keepdims=True) + EPS)
    return y.astype(x.dtype) * g


def layer_norm(x, g, b):
    xf = x.astype(jnp.float32)
    mu = jnp.mean(xf, axis=-1, keepdims=True)
    var = jnp.mean(jnp.square(xf - mu), axis=-1, keepdims=True)
    y = (xf - mu) * lax.rsqrt(var + EPS)
    return y.astype(x.dtype) * g + b


def neighbourhood_attention(q, k, v, rpb):
    B, T, H, dh = q.shape
    rows = T // GRID_W
    kr = min(NA_ROWS_MAX, rows)
    qg = q.reshape(B, rows, GRID_W, H, dh)
    kg = k.reshape(B, rows, GRID_W, H, dh)
    vg = v.reshape(B, rows, GRID_W, H, dh)
    cols = jnp.arange(GRID_W)
    c_start = jnp.clip(cols - NA_COLS // 2, 0, GRID_W - NA_COLS)
    col_idx = c_start[:, None] + jnp.arange(NA_COLS)[None, :]
    dc = col_idx - cols[:, None] + (NA_COLS - 1)
    scale = dh ** -0.5

    def one_row(r):
        r_start = jnp.clip(r - kr // 2, 0, rows - kr)
        k_rows = lax.dynamic_slice_in_dim(kg, r_start, kr, axis=1)
        v_rows = lax.dynamic_slice_in_dim(vg, r_start, kr, axis=1)
        k_win = k_rows[:, :, col_idx]
        v_win = v_rows[:, :, col_idx]
        q_row = lax.dynamic_index_in_dim(qg, r, axis=1, keepdims=False)
        dr = r_start + jnp.arange(kr) - r + (NA_ROWS_MAX - 1)
        bias = rpb[:, dr[:, None, None], dc[None, :, :]]
        bias = bias.transpose(0, 2, 1, 3).astype(jnp.float32)
        s = jnp.einsum('bqhd,bpqkhd->bhqpk', q_row, k_win).astype(jnp.float32) * scale + bias[None]
        p = jax.nn.softmax(s.reshape(B, H, GRID_W, kr * NA_COLS), axis=-1)
        p = p.reshape(B, H, GRID_W, kr, NA_COLS).astype(v.dtype)
        return jnp.einsum('bhqpk,bpqkhd->bqhd', p, v_win)

    out = lax.map(one_row, jnp.arange(rows))
    return out.transpose(1, 0, 2, 3, 4).reshape(B, T, H * dh)


def mixer_block(n, w_in, conv_dw, conv_b, conv_ln_g, conv_ln_b, sgu_ln_g, sgu_ln_b,
                sgu_w, sgu_b, na_rpb, w_br_conv, w_br_sgu, w_br_na, w_out):
    B, T, _ = n.shape
    z = n @ w_in
    a, bg, u, v, q, k, vv, gates = jnp.split(z, np.cumsum(SPLITS).tolist(), axis=-1)
    h = a * jax.nn.sigmoid(bg)
    h = lax.conv_general_dilated(h, conv_dw[:, None, :], (1,),
                                 [(CONV_WIDTH // 2, CONV_WIDTH // 2)],
                                 dimension_numbers=('NWC', 'WIO', 'NWC'),
                                 feature_group_count=D_CONV) + conv_b
    h = jax.nn.silu(layer_norm(h, conv_ln_g, conv_ln_b))
    o_conv = h @ w_br_conv
    vn = layer_norm(v, sgu_ln_g, sgu_ln_b).reshape(B, T // SGU_CHUNK, SGU_CHUNK, SGU_GROUPS, SGU_GROUP_DIM)
    s = jnp.einsum('gij,bnjgc->bnigc', sgu_w, vn) + sgu_b.T[None, None, :, :, None]
    o_sgu = (u * s.reshape(B, T, D_SGU)) @ w_br_sgu
    o_na = neighbourhood_attention(q.reshape(B, T, NA_HEADS, NA_HEAD_DIM),
                                   k.reshape(B, T, NA_HEADS, NA_HEAD_DIM),
                                   vv.reshape(B, T, NA_HEADS, NA_HEAD_DIM), na_rpb) @ w_br_na
    g = jax.nn.sigmoid(gates).reshape(B, T, N_BRANCH, D_MODEL)
    m = g[..., 0, :] * o_conv + g[..., 1, :] * o_sgu + g[..., 2, :] * o_na
    return m @ w_out


def cross_attention(h, mem, norm_mem, wq, wk, wv, wo):
    B, T, _ = h.shape
    M = mem.shape[1]
    mn = rms_norm(mem, norm_mem)
    q = (h @ wq).reshape(B, T, XA_HEADS, XA_HEAD_DIM)
    k = (mn @ wk).reshape(B, M, XA_HEADS, XA_HEAD_DIM)
    v = (mn @ wv).reshape(B, M, XA_HEADS, XA_HEAD_DIM)
    s = jnp.einsum('bthd,bmhd->bhtm', q, k).astype(jnp.float32) * (XA_HEAD_DIM ** -0.5)
    p = jax.nn.softmax(s, axis=-1).astype(v.dtype)
    o = jnp.einsum('bhtm,bmhd->bthd', p, v).reshape(B, T, D_MODEL)
    return o @ wo


def trunk(x, mem, norm_mix, w_in, conv_dw, conv_b, conv_ln_g, conv_ln_b, sgu_ln_g, sgu_ln_b,
          sgu_w, sgu_b, na_rpb, w_br_conv, w_br_sgu, w_br_na, w_out, norm_xa, norm_mem,
          xa_wq, xa_wk, xa_wv, xa_wo, norm_ffn, ffn_w1, ffn_w2, norm_final):
    for l in range(DEPTH):
        x = x + mixer_block(rms_norm(x, norm_mix[l]), w_in[l], conv_dw[l], conv_b[l], conv_ln_g[l],
                            conv_ln_b[l], sgu_ln_g[l], sgu_ln_b[l], sgu_w[l], sgu_b[l], na_rpb[l],
                            w_br_conv[l], w_br_sgu[l], w_br_na[l], w_out[l])
        x = x + cross_attention(rms_norm(x, norm_xa[l]), mem, norm_mem[l],
                                xa_wq[l], xa_wk[l], xa_wv[l], xa_wo[l])
        hf = jnp.square(jax.nn.relu(rms_norm(x, norm_ffn[l]) @ ffn_w1[l]))
        x = x + hf @ ffn_w2[l]
    return rms_norm(x, norm_final)


def setup_inputs(seed: int = 0) -> dict:
    key = jax.random.key(seed)
    ks = iter(jax.random.split(key, 40))
    f32 = jnp.float32

    def nrm(shape, scale):
        return jax.random.normal(next(ks), shape, f32) * scale

    def gain(shape):
        return 1.0 + 0.01 * jax.random.normal(next(ks), shape, f32)

    L, D = DEPTH, D_MODEL
    return {
        "x_prompt": nrm((BATCH, SEQ, D), 1.0),
        "x_sample": nrm((DEC_BATCH, DEC_SEQ, D), 1.0),
        "mem_prompt": nrm((BATCH, N_MEM, D), 1.0),
        "mem_sample": nrm((DEC_BATCH, N_MEM, D), 1.0),
        "norm_mix": gain((L, D)),
        "w_in": nrm((L, D, D_IN), D ** -0.5),
        "conv_dw": nrm((L, CONV_WIDTH, D_CONV), CONV_WIDTH ** -0.5),
        "conv_b": nrm((L, D_CONV), 0.01),
        "conv_ln_g": gain((L, D_CONV)),
        "conv_ln_b": nrm((L, D_CONV), 0.01),
        "sgu_ln_g": gain((L, D_SGU)),
        "sgu_ln_b": nrm((L, D_SGU), 0.01),
        "sgu_w": nrm((L, SGU_GROUPS, SGU_CHUNK, SGU_CHUNK), SGU_CHUNK ** -0.5),
        "sgu_b": gain((L, SGU_GROUPS, SGU_CHUNK)),
        "na_rpb": nrm((L, NA_HEADS, 2 * NA_ROWS_MAX - 1, 2 * NA_COLS - 1), 0.02),
        "w_br_conv": nrm((L, D_CONV, D), D_CONV ** -0.5),
        "w_br_sgu": nrm((L, D_SGU, D), D_SGU ** -0.5),
        "w_br_na": nrm((L, D_NA, D), D_NA ** -0.5),
        "w_out": nrm((L, D, D), D ** -0.5),
        "norm_xa": gain((L, D)),
        "norm_mem": gain((L, D)),
        "xa_wq": nrm((L, D, D), D ** -0.5),
        "xa_wk": nrm((L, D, D), D ** -0.5),
        "xa_wv": nrm((L, D, D), D ** -0.5),
        "xa_wo": nrm((L, D, D), D ** -0.5),
        "norm_ffn": gain((L, D)),
        "ffn_w1": nrm((L, D, D_FF), D ** -0.5),
        "ffn_w2": nrm((L, D_FF, D), D_FF ** -0.5),
        "norm_final": gain((D,)),
    }


def reference(x_prompt, x_sample, mem_prompt, mem_sample, norm_mix, w_in, conv_dw, conv_b,
              conv_ln_g, conv_ln_b, sgu_ln_g, sgu_ln_b, sgu_w, sgu_b, na_rpb, w_br_conv,
              w_br_sgu, w_br_na, w_out, norm_xa, norm_mem, xa_wq, xa_wk, xa_wv, xa_wo,
              norm_ffn, ffn_w1, ffn_w2, norm_final):
    y_prompt = trunk(x_prompt, mem_prompt, norm_mix, w_in, conv_dw, conv_b, conv_ln_g, conv_ln_b,
                     sgu_ln_g, sgu_ln_b, sgu_w, sgu_b, na_rpb, w_br_conv, w_br_sgu, w_br_na, w_out,
                     norm_xa, norm_mem, xa_wq, xa_wk, xa_wv, xa_wo, norm_ffn, ffn_w1, ffn_w2,
                     norm_final)
    y_sample = trunk(x_sample, mem_sample, norm_mix, w_in, conv_dw, conv_b, conv_ln_g, conv_ln_b,
                     sgu_ln_g, sgu_ln_b, sgu_w, sgu_b, na_rpb, w_br_conv, w_br_sgu, w_br_na, w_out,
                     norm_xa, norm_mem, xa_wq, xa_wk, xa_wv, xa_wo, norm_ffn, ffn_w1, ffn_w2,
                     norm_final)
    return (y_prompt, y_sample)
```

```python
import numpy as np
import concourse.bass as bass
import concourse.mybir as mybir
from concourse.bass_utils import run_bass_kernel_spmd

F32 = mybir.dt.float32
BF16 = mybir.dt.bfloat16
AF = mybir.ActivationFunctionType
ALU = mybir.AluOpType
AX = mybir.AxisListType

NCORES = 8
D = 2048
DC = 1024
D_IN = 13312
DFF = 8192
NMEM = 256
CW = 31
EPS = 1e-6
NEG = -30000.0
PIECES_PER_LAYER = 336
ARENA_OFF = 0


class T:
    __slots__ = ("ap", "cells")

    def __init__(self, ap, cells):
        self.ap = ap
        self.cells = cells


class Sched:
    def __init__(self):
        self.ops = []
        self.wr = {}
        self.rd = {}
        self.sig = []

    def add(self, eng, fn, reads=(), writes=(), semkey=None):
        idx = len(self.ops)
        deps = set()
        for t in reads:
            for c in t.cells:
                w = self.wr.get(c)
                if w is not None:
                    deps.add(w)
        for t in writes:
            for c in t.cells:
                w = self.wr.get(c)
                if w is not None:
                    deps.add(w)
                r = self.rd.get(c)
                if r:
                    deps.update(r)
        for t in reads:
            for c in t.cells:
                self.rd.setdefault(c, []).append(idx)
        for t in writes:
            for c in t.cells:
                self.wr[c] = idx
                self.rd[c] = []
        deps.discard(idx)
        self.ops.append((eng, fn, deps, semkey))
        self.sig.append(semkey is not None)
        return idx

    def emit(self, nc, block_ctx_sems):
        ops = self.ops
        n = len(ops)
        sig = self.sig
        for (eng, fn, deps, semkey) in ops:
            for d in deps:
                if ops[d][0] == "pe" and eng == "pe" and ops[d][3] is None:
                    continue
                sig[d] = True
        cnt = {}
        sval = [None] * n
        for i, (eng, fn, deps, semkey) in enumerate(ops):
            if semkey is not None:
                k = "dma_" + semkey
                cnt[k] = cnt.get(k, 0) + (1 if semkey.startswith("cc") else 16)
                sval[i] = (k, cnt[k])
            elif sig[i]:
                k = "eng_" + eng
                cnt[k] = cnt.get(k, 0) + 1
                sval[i] = (k, cnt[k])
        return sval, sorted(cnt.keys())


def build(W, DEPTH, SEAM, DEBUG=False, NCB=NCORES):
    TOK = W * 64
    NS = TOK // 512
    assert W % 8 == 0 and SEAM % 8 == 0
    nc = bass.Bass("TRN2", target_bir_lowering=False)
    S = Sched()

    def din(name, shape, dt=F32):
        return nc.dram_tensor(name, list(shape), dt, kind="ExternalInput")

    xT_in = din("xT", [D, TOK])
    memT_in = din("memT", [2, D, NMEM])
    PPC = PIECES_PER_LAYER // NCB
    wsh = din("wsh", [DEPTH * PPC * 128, 2048])
    gcols_in = din("gcols", [128, (4 * DEPTH + 1) * 16])
    convp_in = din("convp", [DEPTH, 128, 8 * 34])
    sgubc_in = din("sgubc", [DEPTH, 3, 1024])
    sguw_in = din("sguw", [DEPTH, 128, 1024])
    rpbg_in = din("rpbg", [DEPTH, 128, 8 * 15 * 64])
    cmask_in = din("cmask", [128, 8 * 15 * 64])
    seam_in = din("seam", [8, 1024])
    flag_in = din("flag", [128, 1])
    yT_out = nc.dram_tensor("yT", [D, TOK], F32, kind="ExternalOutput")

    wb = [nc.dram_tensor(f"wb{l}", [PPC * 128, 2048], BF16) for l in range(DEPTH)]
    wall = [nc.dram_tensor(f"wall{l}", [PIECES_PER_LAYER * 128, 2048], BF16) for l in range(DEPTH)]
    dk = {}
    xT_d = nc.dram_tensor("xT_d", [D, TOK], F32, **dk)
    zab_d = nc.dram_tensor("zab_d", [2048, TOK], BF16, **dk)
    qk_d = nc.dram_tensor("qk_d", [2048, TOK], BF16, **dk)
    vna_d = nc.dram_tensor("vna_d", [TOK, 1024], BF16, **dk)
    gates_d = nc.dram_tensor("gates_d", [6144, TOK], BF16, **dk)
    us_d = nc.dram_tensor("us_d", [1024, TOK], BF16, **dk)
    if DEBUG:
        dk = dict(kind="ExternalOutput")
        dbg_h = nc.dram_tensor("dbg_h", [1024, TOK], BF16, **dk)
        dbg_o = nc.dram_tensor("dbg_o", [1024, TOK], BF16, **dk)
        dbg_m = nc.dram_tensor("dbg_m", [2048, TOK], BF16, **dk)
        dbg_xc = nc.dram_tensor("dbg_xc", [2048, TOK], F32, **dk)
        dbg_xd = nc.dram_tensor("dbg_xd", [2048, TOK], F32, **dk)

    import contextlib
    es = contextlib.ExitStack()
    with es:
        SB_BYTES = 200 * 1024
        sb = es.enter_context(nc.sbuf_tensor("sb", [128, SB_BYTES // 2], BF16))
        psF = es.enter_context(nc.psum_tensor("psF", [128, 6, 512], F32))
        psB = es.enter_context(nc.psum_tensor("psB", [128, 2, 1024], BF16))

        def V(off, dt, shape):
            esz = 4 if dt == F32 else 2
            nel = int(np.prod(shape))
            assert off % 4 == 0 and off + nel * esz <= SB_BYTES, (off, nel, esz)
            a = sb[:, off // 2: off // 2 + nel * esz // 2]
            if dt == F32:
                a = a.bitcast(F32)
            if len(shape) == 2:
                a = a.rearrange("p (a b) -> p a b", b=shape[1])
            elif len(shape) == 3:
                a = a.rearrange("p (a b c) -> p a b c", b=shape[1], c=shape[2])
            cells = [("s", c) for c in range(off // 1024, (off + nel * esz - 1) // 1024 + 1)]
            return T(a, cells)

        def sub(t, ap, lo_frac=None):
            return T(ap, t.cells)

        def PS(b):
            return T(psF[:, b, :], [("pf", b)])

        def PB(b):
            return T(psB[:, b, :], [("pb", b)])

        def DR(name, key):
            return [("d", name, key)]

        o = 0
        def alloc(nbytes):
            nonlocal o
            r = o
            o += (nbytes + 1023) // 1024 * 1024
            return r
        XT = alloc(32768)
        KM = alloc(8192)
        VM = alloc(8192)
        NW = 5
        WR = [alloc(4096) for _ in range(NW)]
        CST = alloc(1024)
        GC = alloc((4 * DEPTH + 1) * 64)
        CVP = alloc(8 * 34 * 4)
        LNG = alloc(4096); LNB = alloc(4096); SGB = alloc(4096); SGW = alloc(2048)
        BF = alloc(15360)
        FLG = alloc(64)
        AR = o
        assert AR + 96 * 1024 <= SB_BYTES, AR

        xT = V(XT, F32, [16, 512])
        def xTc(fc):
            return T(xT.ap[:, fc, :], [("s", (XT + fc * 2048) // 1024), ("s", (XT + fc * 2048) // 1024 + 1)])
        KmT = V(KM, BF16, [16, 256])
        Vm = V(VM, BF16, [2, 2048])
        ones2048 = V(CST, BF16, [128]); ones1024 = V(CST + 256, BF16, [128]); ident = V(CST + 512, BF16, [128])
        gcols = V(GC, F32, [4 * DEPTH + 1, 16])
        convp = V(CVP, F32, [8, 34])
        lng = V(LNG, F32, [1024]); lnb = V(LNB, F32, [1024]); sgb = V(SGB, F32, [1024]); sgw = V(SGW, BF16, [8, 128])
        Bf = V(BF, BF16, [8, 15 * 64])
        flag = V(FLG, F32, [1])
        epsc = V(FLG + 4, F32, [1])

        def A(off, dt, shape):
            return V(AR + off, dt, shape)

        def chunk(base_off, dt, n1, i):
            esz = 4 if dt == F32 else 2
            return A(base_off + i * n1 * esz, dt, [n1])

        st = {"ps": 0, "pb": 0, "w": 0, "piece": {}}

        def next_ps():
            b = st["ps"] % 6
            st["ps"] += 1
            return b

        def next_pb():
            b = st["pb"] % 2
            st["pb"] += 1
            return b

        def next_piece(l):
            i = st["piece"].get(l, 0)
            st["piece"][l] = i + 1
            slot = st["w"] % NW
            st["w"] += 1
            wt = V(WR[slot], BF16, [2048])
            src = wall[l][(i % PIECES_PER_LAYER) * 128:(i % PIECES_PER_LAYER + 1) * 128, :]
            S.add("sp", lambda e, wt=wt, src=src: e.dma_start(out=wt.ap, in_=src),
                  reads=[T(None, DR("wall", l))], writes=[wt], semkey=f"w{slot}")
            return wt

        def set_piece(l, i):
            st["piece"][l] = i

        P_U, P_VS, P_AB, P_QK, P_VN, P_G = 0, 8, 16, 32, 48, 56
        P_BR, P_WOUT, P_WQ, P_WK, P_WV, P_WO, P_W1, P_W2 = 104, 128, 144, 160, 176, 192, 208, 272

        def dve(fn, reads, writes):
            return S.add("dve", fn, reads, writes)

        def act(fn, reads, writes):
            return S.add("act", fn, reads, writes)

        def pe(fn, reads, writes):
            return S.add("pe", fn, reads, writes)

        def dma(fn, reads, writes, key, eng="pool"):
            return S.add(eng, fn, reads, writes, semkey=key)

        def rsqrt_eps(out, in_):
            act(lambda e: e.activation(out.ap, in_.ap, AF.Sqrt, bias=epsc.ap[:, 0:1]), [in_, epsc], [out])
            dve(lambda e: e.reciprocal(out.ap, out.ap), [out], [out])

        evac_rr = [0]

        def mm_group(ps, pairs):
            n = len(pairs)
            reads = []
            for a, b in pairs:
                reads.append(a); reads.append(b)

            def fn(e, ps=ps, pairs=pairs, n=n):
                ins = None
                for i, (a, b) in enumerate(pairs):
                    ins = e.matmul(ps.ap, a.ap, b.ap, start=(i == 0), stop=(i == n - 1))
                return ins
            pe(fn, reads, [ps])

        def rmsnorm2(gl, out_off, tmp_off):
            ps = PS(next_ps())
            sqs = [A(tmp_off + k * 1024, BF16, [512]) for k in range(2)]
            rstd = A(tmp_off + 2048, F32, [512])
            for fc in range(16):
                sq = sqs[fc % 2]
                act(lambda e, sq=sq, fc=fc: e.activation(sq.ap, xT.ap[:, fc, :], AF.Square), [xTc(fc)], [sq])
                S.add("pe", lambda e, sq=sq, fc=fc, ps=ps: e.matmul(ps.ap, ones2048.ap, sq.ap, start=(fc == 0), stop=(fc == 15)),
                      [sq, ones2048] + ([ps] if fc else []), [ps])
            rsqrt_eps(rstd, ps)
            outs = []
            for fc in range(16):
                ot = chunk(out_off, BF16, 512, fc)
                dve(lambda e, ot=ot, fc=fc: e.scalar_tensor_tensor(ot.ap, xT.ap[:, fc, :], gcols.ap[:, gl, fc:fc + 1], rstd.ap,
                                                                    ALU.mult, ALU.mult), [xTc(fc), gcols, rstd], [ot])
                outs.append(ot)
            return outs

        def evac_copy(dst, ps, scale=None, func=None):
            if func is None and scale is None and evac_rr[0] % 2 == 0:
                dve(lambda e: e.tensor_copy(dst.ap, ps.ap), [ps], [dst])
            else:
                f = func if func is not None else AF.Copy
                if scale is None:
                    act(lambda e: e.activation(dst.ap, ps.ap, f), [ps], [dst])
                else:
                    act(lambda e: e.activation(dst.ap, ps.ap, f, scale=scale), [ps], [dst])
            evac_rr[0] += 1

        def load_block_x(l, s):
            src = (xT_in if l == 0 else xT_d).ap().rearrange("(fc p) t -> p fc t", p=128)[:, :, s * 512:(s + 1) * 512]
            rd = [] if l == 0 else [T(None, DR("xT_d", s))]
            dma(lambda e: e.dma_start(out=xT.ap, in_=src), rd, [xT], "xT")

        def setup():
            S.add("pool", lambda e: e.memset(ones2048.ap, 1.0 / D), [], [ones2048])
            S.add("pool", lambda e: e.memset(ones1024.ap, 1.0 / DC), [], [ones1024])
            S.add("pool", lambda e: e.memset(epsc.ap, EPS), [], [epsc])
            dma(lambda e: e.dma_start(out=gcols.ap, in_=gcols_in.ap().rearrange("p (a b) -> p a b", b=16)), [], [gcols], "k_gcols")
            dma(lambda e: e.dma_start(out=flag.ap, in_=flag_in.ap()), [], [flag], "k_flag")
            NCH = DEPTH * PPC
            for i in range(NCH):
                dst = (wb if NCB > 1 else wall)[i // PPC]
                S.add("pool", lambda e, i=i, dst=dst: e.dma_start(out=dst[(i % PPC) * 128:(i % PPC + 1) * 128, :], in_=wsh[i * 128:(i + 1) * 128, :]),
                      [], [T(None, DR("wb" if NCB > 1 else "wall", i // PPC))], semkey=f"cast{i // PPC}")
            for l in range(DEPTH if NCB > 1 else 0):
                S.add("pool", lambda e, l=l: e.collective_compute(
                    "AllGather", ALU.bypass, replica_groups=[list(range(NCORES))],
                    ins=[wb[l].ap().opt()], outs=[wall[l].ap().opt()]),
                    [T(None, DR("wb", l))], [T(None, DR("wall", l))], semkey=f"cc{l}")

        identF = [None]

        def layer_consts(l):
            dma(lambda e: e.dma_start(out=convp.ap, in_=convp_in[l].rearrange("p (a b) -> p a b", b=34)), [], [convp], "k_convp")
            dma(lambda e: e.dma_start(out=lng.ap, in_=sgubc_in[l, 0:1, :].partition_broadcast(128)), [], [lng], "k_lng")
            dma(lambda e: e.dma_start(out=lnb.ap, in_=sgubc_in[l, 1:2, :].partition_broadcast(128)), [], [lnb], "k_lnb")
            dma(lambda e: e.dma_start(out=sgb.ap, in_=sgubc_in[l, 2:3, :].partition_broadcast(128)), [], [sgb], "k_sgb")
            tw = A(0, F32, [1024])
            dma(lambda e: e.dma_start(out=tw.ap, in_=sguw_in[l]), [], [tw], "k_tw")
            dve(lambda e: e.tensor_copy(sgw.ap.rearrange("p a b -> p (a b)"), tw.ap), [tw], [sgw])
            for q in range(4):
                t1 = A(4096 + (q % 2) * 16384, F32, [1920])
                t2 = A(4096 + (q % 2) * 16384 + 8192, F32, [1920])
                dma(lambda e, t1=t1, q=q: e.dma_start(out=t1.ap, in_=rpbg_in[l][:, q * 1920:(q + 1) * 1920]), [], [t1], f"k_t1{q % 2}")
                dma(lambda e, t2=t2, q=q: e.dma_start(out=t2.ap, in_=cmask_in[:, q * 1920:(q + 1) * 1920]), [], [t2], f"k_t2{q % 2}")
                bo = T(Bf.ap.rearrange("p a b -> p (a b)")[:, q * 1920:(q + 1) * 1920], Bf.cells)
                dve(lambda e, t1=t1, t2=t2, bo=bo: e.tensor_tensor(bo.ap, t1.ap, t2.ap, ALU.add), [t1, t2], [bo])

        def mem_kv(l, m):
            mT = A(0, F32, [16, 256])
            mn = A(16384, BF16, [16, 256])
            sqr = [A(24576 + k * 512, BF16, [256]) for k in range(2)]
            rstd = A(25600, F32, [256])
            dma(lambda e: e.dma_start(out=mT.ap, in_=memT_in[m].rearrange("(fc p) t -> p fc t", p=128)), [], [mT], "mem")
            ps = PS(next_ps())
            psv = T(ps.ap[:, 0:256], ps.cells)
            for fc in range(16):
                sq = sqr[fc % 2]
                act(lambda e, sq=sq, fc=fc: e.activation(sq.ap, mT.ap[:, fc, :], AF.Square), [mT], [sq])
                S.add("pe", lambda e, sq=sq, fc=fc, psv=psv: e.matmul(psv.ap, ones2048.ap, sq.ap, start=(fc == 0), stop=(fc == 15)),
                      [sq, ones2048] + ([psv] if fc else []), [psv])
            rsqrt_eps(rstd, psv)
            gl = 2 * DEPTH + l
            for fc in range(16):
                dve(lambda e, fc=fc: e.scalar_tensor_tensor(mn.ap[:, fc, :], mT.ap[:, fc, :], gcols.ap[:, gl, fc:fc + 1], rstd.ap,
                                                            ALU.mult, ALU.mult), [mT, gcols, rstd], [mn])
            set_piece(l, P_WK)
            for oc in range(16):
                wt = next_piece(l)
                ps = PS(next_ps())
                psv = T(ps.ap[:, 0:256], ps.cells)
                w3 = wt.ap.rearrange("p (k n) -> p k n", n=128)
                mm_group(psv, [(T(w3[:, kc, :], wt.cells), T(mn.ap[:, kc, :], mn.cells)) for kc in range(16)])
                evac_copy(T(KmT.ap[:, oc, :], KmT.cells), psv)
            for cb in range(4):
                pss = [PS(next_ps()) for _ in range(2)]
                for kg in range(4):
                    wt = next_piece(l)
                    w3 = wt.ap.rearrange("p (k n) -> p k n", n=512)
                    for mc in range(2):
                        def fn(e, w3=w3, kg=kg, mc=mc, ps=pss[mc]):
                            ins = None
                            for kc in range(4):
                                ins = e.matmul(ps.ap, mn.ap[:, kg * 4 + kc, mc * 128:(mc + 1) * 128], w3[:, kc, :],
                                               start=(kg == 0 and kc == 0), stop=(kg == 3 and kc == 3))
                            return ins
                        S.add("pe", fn, [wt, mn] + ([pss[mc]] if kg else []), [pss[mc]])
                for mc in range(2):
                    evac_copy(T(Vm.ap[:, mc, cb * 512:(cb + 1) * 512], Vm.cells), pss[mc])

        def phase_a(l, s):
            t0 = s * 512
            load_block_x(l, s)
            nT = rmsnorm2(l, 0, 49152)
            set_piece(l, 0)
            stg_i = [0]

            def stage():
                k = stg_i[0] % 4
                stg_i[0] += 1
                return A(53248 + k * 1024, BF16, [512]), f"stg{k}"

            def typeA(dst_fn):
                wt = next_piece(l)
                ps = PS(next_ps())
                w3 = wt.ap.rearrange("p (k n) -> p k n", n=128)
                mm_group(ps, [(T(w3[:, kc, :], wt.cells), nT[kc]) for kc in range(16)])
                dst_fn(ps)

            def typeB(evac_fn):
                pss = [PS(next_ps()) for _ in range(4)]
                for kg in range(4):
                    wt = next_piece(l)
                    w3 = wt.ap.rearrange("p (k n) -> p k n", n=512)
                    for tt in range(4):
                        def fn(e, w3=w3, kg=kg, tt=tt, ps=pss[tt]):
                            ins = None
                            for kc in range(4):
                                ins = e.matmul(ps.ap, nT[kg * 4 + kc].ap[:, tt * 128:(tt + 1) * 128], w3[:, kc, :],
                                               start=(kg == 0 and kc == 0), stop=(kg == 3 and kc == 3))
                            return ins
                        S.add("pe", fn, [wt] + [nT[kg * 4 + kc] for kc in range(4)] + ([pss[tt]] if kg else []), [pss[tt]])
                for tt in range(4):
                    evac_fn(tt, pss[tt])

            uT = [chunk(16384, BF16, 512, c) for c in range(8)]
            for c in range(8):
                typeA(lambda ps, c=c: evac_copy(uT[c], ps))
            vtok = [A(24576 + tt * 4096, F32, [1024]) for tt in range(4)]
            for cb in range(2):
                typeB(lambda tt, ps, cb=cb: evac_copy(T(vtok[tt].ap[:, cb * 512:(cb + 1) * 512], vtok[tt].cells), ps))
            usT = [chunk(69632, BF16, 512, c) for c in range(8)]
            for tt in range(4):
                stt = A(77824, F32, [16])
                mv = A(77824 + 64, F32, [4])
                vt = vtok[tt]
                dve(lambda e, vt=vt: e.bn_stats(stt.ap[:, 0:6], vt.ap[:, 0:512]), [vt], [stt])
                dve(lambda e, vt=vt: e.bn_stats(stt.ap[:, 6:12], vt.ap[:, 512:1024]), [vt, stt], [stt])
                dve(lambda e: e.bn_aggr(mv.ap[:, 0:2], stt.ap[:, 0:12]), [stt], [mv])
                rsqrt_eps(T(mv.ap[:, 2:3], mv.cells), T(mv.ap[:, 1:2], mv.cells))
                dve(lambda e, vt=vt: e.tensor_scalar(vt.ap, vt.ap, mv.ap[:, 0:1], mv.ap[:, 2:3], ALU.subtract, ALU.mult), [vt, mv], [vt])
                dve(lambda e, vt=vt: e.tensor_tensor(vt.ap, vt.ap, lng.ap, ALU.mult), [vt, lng], [vt])
                vn = A(40960 + tt * 2048, BF16, [1024])
                dve(lambda e, vt=vt, vn=vn: e.tensor_tensor(vn.ap, vt.ap, lnb.ap, ALU.add), [vt, lnb], [vn])
                for half in range(2):
                    ps = PS(next_ps())

                    def fn(e, ps=ps, vn=vn, half=half):
                        ins = None
                        for gg in range(4):
                            g = half * 4 + gg
                            ins = e.matmul(ps.ap[:, gg * 128:(gg + 1) * 128], vn.ap[:, g * 128:(g + 1) * 128], sgw.ap[:, g, :],
                                           start=True, stop=True)
                        return ins
                    S.add("pe", fn, [vn, sgw], [ps])
                    tmp = A(65536 + half * 2048, F32, [512])
                    dve(lambda e, ps=ps, tmp=tmp, half=half: e.tensor_tensor(tmp.ap, ps.ap, sgb.ap[:, half * 512:(half + 1) * 512], ALU.add),
                        [ps, sgb], [tmp])
                    for gg in range(4):
                        g = half * 4 + gg
                        dve(lambda e, tmp=tmp, gg=gg, g=g, tt=tt: e.tensor_tensor(usT[g].ap[:, tt * 128:(tt + 1) * 128], tmp.ap[:, gg * 128:(gg + 1) * 128],
                                                                                  uT[g].ap[:, tt * 128:(tt + 1) * 128], ALU.mult),
                            [tmp, uT[g]], [usT[g]])
            usall = A(69632, BF16, [8, 512])
            dma(lambda e: e.dma_start(out=us_d.ap().rearrange("(c p) t -> p c t", p=128)[:, :, t0:t0 + 512], in_=usall.ap),
                [usall], [T(None, DR("us_d", s))], "usst")

            def store_chunk(dram, row0, key, scale=None, func=None):
                def dst(ps):
                    sg, k = stage()
                    evac_copy(sg, ps, scale=scale, func=func)
                    dma(lambda e: e.dma_start(out=dram[row0:row0 + 128, t0:t0 + 512], in_=sg.ap), [sg], [T(None, DR(key, s))], k)
                return dst
            for c in range(16):
                typeA(store_chunk(zab_d, c * 128, "zab_d"))
            for c in range(16):
                typeA(store_chunk(qk_d, c * 128, "qk_d", scale=(0.125 if c < 8 else None)))
            for cb in range(2):
                vst = A(57344, BF16, [4, 512])

                def ev(tt, ps, cb=cb, vst=vst):
                    evac_copy(T(vst.ap[:, tt, :], vst.cells), ps)
                typeB(ev)
                dma(lambda e, cb=cb, vst=vst: e.dma_start(
                    out=vna_d.ap().rearrange("(tt p) c -> p tt c", p=128)[:, s * 4:(s + 1) * 4, cb * 512:(cb + 1) * 512], in_=vst.ap),
                    [vst], [T(None, DR("vna_d", s))], "vst")
            for c in range(48):
                typeA(store_chunk(gates_d, c * 128, "gates_d", func=AF.Sigmoid))

        def na_row(l, s, r, qT, onaT):
            seam_row = (SEAM - 4 <= r < SEAM + 4) and 0 < SEAM < W
            if seam_row:
                kr0, nr = SEAM - 8, 16
            else:
                kr0, nr = min(max(r - 4, 0), W - 8), 8
            nk = nr * 64
            nch = nk // 128
            slot = r % 2 if not seam_row else 0
            if seam_row:
                Kw = A(0, BF16, [8, 1024]); Vw = A(16384, BF16, [8, 1024])
                tt_ = A(40960, F32, [1024]); pp = A(45056, BF16, [1024]); pT = A(47104, BF16, [8, 128])
            else:
                Kw = A(slot * 8192, BF16, [8, 512]); Vw = A(16384 + slot * 8192, BF16, [4, 1024])
                tt_ = None
            k0 = kr0 * 64
            dma(lambda e: e.dma_start(out=Kw.ap, in_=qk_d.ap().rearrange("(c p) t -> p c t", p=128)[:, 8:16, k0:k0 + nk]),
                [T(None, DR("qk_d", j)) for j in range(k0 // 512, (k0 + nk - 1) // 512 + 1)], [Kw], f"kw{slot}")
            dma(lambda e: e.dma_start(out=Vw.ap, in_=vna_d[k0:k0 + nk, :].rearrange("(c p) d -> p c d", p=128)),
                [T(None, DR("vna_d", j)) for j in range(k0 // 512, (k0 + nk - 1) // 512 + 1)], [Vw], f"vw{slot}")
            if seam_row:
                smk = A(77056, F32, [1024])
                ri = r - (SEAM - 4)
                dma(lambda e: e.dma_start(out=smk.ap, in_=seam_in[ri:ri + 1, :].partition_broadcast(128)), [], [smk], "smk")
            qc0 = (r % 8) * 64
            stats = A(49152, F32, [16])
            for hp in range(8):
                it = (r * 8 + hp) % 2
                if not seam_row:
                    tt_ = A(40960 + it * 2048, F32, [512]); pp = A(45056 + it * 1024, BF16, [512]); pT = A(47104 + it * 1024, BF16, [4, 128])
                st4 = T(stats.ap[:, it * 4:(it + 1) * 4], stats.cells)
                pss = [PS(next_ps()) for _ in range(nk // 512)]
                for j, ps in enumerate(pss):
                    def fn(e, ps=ps, j=j, hp=hp):
                        e.matmul(ps.ap[0:64, :], qT.ap[0:64, hp, qc0:qc0 + 64], Kw.ap[0:64, hp, j * 512:(j + 1) * 512], start=True, stop=True)
                        return e.matmul(ps.ap[64:128, :], qT.ap[64:128, hp, qc0:qc0 + 64], Kw.ap[64:128, hp, j * 512:(j + 1) * 512], start=True, stop=True)
                    S.add("pe", fn, [qT, Kw], [ps])
                if not seam_row:
                    dr0 = kr0 - r + 7
                    dve(lambda e, ps=pss[0], hp=hp, dr0=dr0, tt_=tt_: e.tensor_tensor(tt_.ap, ps.ap, Bf.ap[:, hp, dr0 * 64:(dr0 + 8) * 64], ALU.add),
                        [pss[0], Bf], [tt_])
                else:
                    krA, krB = max(kr0, r - 7), min(kr0 + nr, r + 8)
                    for j, ps in enumerate(pss):
                        ra, rb = kr0 + j * 8, kr0 + j * 8 + 8
                        va, vb = max(ra, krA), min(rb, krB)
                        if va < vb:
                            dve(lambda e, ps=ps, j=j, va=va, vb=vb, ra=ra, hp=hp: e.tensor_tensor(
                                tt_.ap[:, j * 512 + (va - ra) * 64: j * 512 + (vb - ra) * 64], ps.ap[:, (va - ra) * 64:(vb - ra) * 64],
                                Bf.ap[:, hp, (va - r + 7) * 64:(vb - r + 7) * 64], ALU.add), [ps, Bf], [tt_])
                        for (xa, xb) in ((ra, min(rb, va) if va < vb else rb), (max(ra, vb) if va < vb else rb, rb)):
                            if xa < xb:
                                dve(lambda e, ps=ps, j=j, xa=xa, xb=xb, ra=ra: e.tensor_scalar(
                                    tt_.ap[:, j * 512 + (xa - ra) * 64: j * 512 + (xb - ra) * 64], ps.ap[:, (xa - ra) * 64:(xb - ra) * 64],
                                    NEG, None, ALU.add), [ps], [tt_])
                    dve(lambda e: e.tensor_tensor(tt_.ap, tt_.ap, smk.ap, ALU.add), [tt_, smk], [tt_])
                tl = tt_
                dve(lambda e, tl=tl, st4=st4: e.reduce_max(st4.ap[:, 0:1], tl.ap, AX.X), [tl], [st4])
                dve(lambda e, st4=st4: e.tensor_scalar(st4.ap[:, 1:2], st4.ap[:, 0:1], -1.0, None, ALU.mult), [st4], [st4])
                ppl = pp
                act(lambda e, tl=tl, ppl=ppl, st4=st4: e.activation(ppl.ap, tl.ap, AF.Exp, bias=st4.ap[:, 1:2]), [tl, st4], [ppl])
                dve(lambda e, ppl=ppl, st4=st4: e.reduce_sum(st4.ap[:, 2:3], ppl.ap, AX.X), [ppl], [st4])
                dve(lambda e, st4=st4: e.reciprocal(st4.ap[:, 3:4], st4.ap[:, 2:3]), [st4], [st4])
                dve(lambda e, ppl=ppl, st4=st4: e.tensor_scalar(ppl.ap, ppl.ap, st4.ap[:, 3:4], None, ALU.mult), [ppl, st4], [ppl])
                pb = PB(next_pb())

                def tfn(e, pb=pb, ppl=ppl):
                    ins = None
                    for kc in range(nch):
                        ins = e.transpose(pb.ap[:, kc * 128:(kc + 1) * 128], ppl.ap[:, kc * 128:(kc + 1) * 128], ident.ap)
                    return ins
                S.add("pe", tfn, [ppl, ident], [pb])
                pTl = pT
                act(lambda e, pb=pb, pTl=pTl: e.copy(pTl.ap.rearrange("p a b -> p (a b)"), pb.ap[:, 0:nch * 128]), [pb], [pTl])
                po = PS(next_ps())

                def ofn(e, po=po, pTl=pTl, hp=hp):
                    ins = None
                    for kc in range(nch):
                        ins = e.matmul(po.ap[:, 0:128], Vw.ap[:, kc, hp * 128:(hp + 1) * 128],
                                       pTl.ap[:, kc, :], start=(kc == 0), stop=(kc == nch - 1))
                    return ins
                S.add("pe", ofn, [Vw, pTl], [po])

                def cfn(e, po=po, hp=hp):
                    e.copy(onaT[hp].ap[0:64, qc0:qc0 + 64], po.ap[0:64, 0:64])
                    return e.copy(onaT[hp].ap[64:128, qc0:qc0 + 64], po.ap[64:128, 64:128])
                act(cfn, [po], [onaT[hp]])

        def phase_b(l, s, last):
            t0 = s * 512
            load_block_x(l, s)
            acc = [chunk(21760, F32, 512, c) for c in range(8)]
            hg = [None] * 8
            lo, hi = t0 - 15, t0 + 512 + 15
            clo, chi = max(lo, 0), min(hi, TOK)
            for half in range(2):
                for c4 in range(4):
                    cc = half * 4 + c4
                    ab = A(8704 + c4 * 2176, BF16, [2, 544])
                    sg = A(17408 + (c4 % 2) * 2176, F32, [544])
                    h = A(c4 * 2176, F32, [544])
                    hg[cc] = h
                    if clo > lo:
                        S.add("pool", lambda e, ab=ab: e.memset(ab.ap[:, :, 0:clo - lo], 0.0), [], [ab])
                    if chi < hi:
                        S.add("pool", lambda e, ab=ab: e.memset(ab.ap[:, :, chi - lo:542], 0.0), [], [ab])
                    dma(lambda e, ab=ab, cc=cc: e.dma_start(out=ab.ap[:, :, clo - lo:chi - lo],
                                                           in_=zab_d.ap().rearrange("(two c p) t -> p two c t", two=2, p=128)[:, :, cc, clo:chi]),
                        [T(None, DR("zab_d", j)) for j in range(clo // 512, (chi - 1) // 512 + 1)], [ab], f"ab{c4}")
                    act(lambda e, ab=ab, sg=sg: e.activation(sg.ap[:, 0:542], ab.ap[:, 1, 0:542], AF.Sigmoid), [ab], [sg])
                    dve(lambda e, ab=ab, sg=sg, h=h: e.tensor_tensor(h.ap[:, 0:542], ab.ap[:, 0, 0:542], sg.ap[:, 0:542], ALU.mult), [ab, sg], [h])
                    if 0 < SEAM < W and t0 + 512 == SEAM * 64:
                        dve(lambda e, h=h: e.tensor_scalar(h.ap[:, 527:542], h.ap[:, 527:542], flag.ap[:, 0:1], None, ALU.mult), [h, flag], [h])
                    if 0 < SEAM < W and t0 == SEAM * 64:
                        dve(lambda e, h=h: e.tensor_scalar(h.ap[:, 0:15], h.ap[:, 0:15], flag.ap[:, 0:1], None, ALU.mult), [h, flag], [h])
                for j in range(CW):
                    for c4 in range(4):
                        cc = half * 4 + c4
                        h = hg[cc]
                        if j == 0:
                            dve(lambda e, h=h, cc=cc: e.tensor_scalar(acc[cc].ap, h.ap[:, 0:512], convp.ap[:, cc, 0:1], convp.ap[:, cc, 31:32],
                                                                      ALU.mult, ALU.add), [h, convp], [acc[cc]])
                        else:
                            dve(lambda e, h=h, cc=cc, j=j: e.scalar_tensor_tensor(acc[cc].ap, h.ap[:, j:j + 512], convp.ap[:, cc, j:j + 1], acc[cc].ap,
                                                                                  ALU.mult, ALU.add), [h, convp, acc[cc]], [acc[cc]])
            psm = PS(next_ps()); psq = PS(next_ps())
            for cc in range(8):
                ab_ = A(38144 + (cc % 2) * 2048, BF16, [512]); sq_ = A(38144 + (cc % 2) * 2048 + 1024, BF16, [512])
                act(lambda e, ab_=ab_, cc=cc: e.copy(ab_.ap, acc[cc].ap), [acc[cc]], [ab_])
                act(lambda e, sq_=sq_, cc=cc: e.activation(sq_.ap, acc[cc].ap, AF.Square), [acc[cc]], [sq_])
                S.add("pe", lambda e, ab_=ab_, cc=cc: e.matmul(psm.ap, ones1024.ap, ab_.ap, start=(cc == 0), stop=(cc == 7)),
                      [ab_, ones1024] + ([psm] if cc else []), [psm])
                S.add("pe", lambda e, sq_=sq_, cc=cc: e.matmul(psq.ap, ones1024.ap, sq_.ap, start=(cc == 0), stop=(cc == 7)),
                      [sq_, ones1024] + ([psq] if cc else []), [psq])
            mean = A(42240, F32, [512]); rstd = A(44288, F32, [512]); tmp = A(46336, F32, [512])
            dve(lambda e: e.tensor_copy(mean.ap, psm.ap), [psm], [mean])
            dve(lambda e: e.tensor_tensor(tmp.ap, mean.ap, mean.ap, ALU.mult), [mean], [tmp])
            dve(lambda e: e.tensor_tensor(tmp.ap, psq.ap, tmp.ap, ALU.subtract), [psq, tmp], [tmp])
            dve(lambda e: e.tensor_scalar(tmp.ap, tmp.ap, 0.0, None, ALU.max), [tmp], [tmp])
            rsqrt_eps(rstd, tmp)
            hcnT = [chunk(52480, BF16, 512, c) for c in range(8)]
            for cc in range(8):
                y = A(48384 + (cc % 2) * 2048, F32, [512])
                dve(lambda e, y=y, cc=cc, mean=mean: e.tensor_tensor(y.ap, acc[cc].ap, mean.ap, ALU.subtract), [acc[cc], mean], [y])
                dve(lambda e, y=y, rstd=rstd: e.tensor_tensor(y.ap, y.ap, rstd.ap, ALU.mult), [y, rstd], [y])
                dve(lambda e, y=y, cc=cc: e.tensor_scalar(y.ap, y.ap, convp.ap[:, cc, 32:33], convp.ap[:, cc, 33:34], ALU.mult, ALU.add), [y, convp], [y])
                act(lambda e, y=y, cc=cc: e.activation(hcnT[cc].ap, y.ap, AF.Silu), [y], [hcnT[cc]])
            usall = A(60672, BF16, [8, 512])
            usT = [chunk(60672, BF16, 512, c) for c in range(8)]
            dma(lambda e: e.dma_start(out=usall.ap, in_=us_d.ap().rearrange("(c p) t -> p c t", p=128)[:, :, t0:t0 + 512]),
                [T(None, DR("us_d", s))], [usall], "usld")
            qT = A(32768, BF16, [8, 512])
            dma(lambda e: e.dma_start(out=qT.ap, in_=qk_d.ap().rearrange("(c p) t -> p c t", p=128)[:, 0:8, t0:t0 + 512]),
                [T(None, DR("qk_d", s))], [qT], "qld")
            onaT = [chunk(68864, BF16, 512, c) for c in range(8)]
            for r in range(s * 8, s * 8 + 8):
                na_row(l, s, r, qT, onaT)
            if DEBUG:
                hall = A(52480, BF16, [8, 512]); oall = A(68864, BF16, [8, 512])
                dma(lambda e: e.dma_start(out=dbg_h.ap().rearrange("(c p) t -> p c t", p=128)[:, :, t0:t0 + 512], in_=hall.ap), [hall], [T(None, DR("dbg_h", s))], "dbg0")
                dma(lambda e: e.dma_start(out=dbg_o.ap().rearrange("(c p) t -> p c t", p=128)[:, :, t0:t0 + 512], in_=oall.ap), [oall], [T(None, DR("dbg_o", s))], "dbg1")
            set_piece(l, P_BR)
            mT = [chunk(77056, BF16, 512, c) for c in range(16)]
            srcs = [hcnT, usT, onaT]
            for ocp in range(8):
                wts = [next_piece(l) for _ in range(3)]
                for o2 in range(2):
                    oc = ocp * 2 + o2
                    gt = A((oc % 2) * 3072, BF16, [3, 512])
                    dma(lambda e, gt=gt, oc=oc: e.dma_start(out=gt.ap, in_=gates_d.ap().rearrange("(b c p) t -> p b c t", b=3, p=128)[:, :, oc, t0:t0 + 512]),
                        [T(None, DR("gates_d", s))], [gt], f"gt{oc % 2}")
                    tms = [A(6144 + b * 2048, F32, [512]) for b in range(3)]
                    for b in range(3):
                        ps = PS(next_ps())
                        w4 = wts[b].ap.rearrange("p (o k n) -> p o k n", o=2, n=128)
                        mm_group(ps, [(T(w4[:, o2, kc, :], wts[b].cells), srcs[b][kc]) for kc in range(8)])
                        dve(lambda e, ps=ps, b=b, gt=gt, tm=tms[b]: e.tensor_tensor(tm.ap, ps.ap, gt.ap[:, b, :], ALU.mult), [ps, gt], [tms[b]])
                    dve(lambda e, tms=tms: e.tensor_tensor(tms[0].ap, tms[0].ap, tms[1].ap, ALU.add), [tms[0], tms[1]], [tms[0]])
                    dve(lambda e, tms=tms, oc=oc: e.tensor_tensor(mT[oc].ap, tms[0].ap, tms[2].ap, ALU.add), [tms[0], tms[2]], [mT[oc]])

            def proj_residual(src, nk_pieces=1):
                for oc in range(16):
                    ps = PS(next_ps())
                    pairs = []
                    for kg in range(nk_pieces):
                        wt = next_piece(l)
                        w3 = wt.ap.rearrange("p (k n) -> p k n", n=128)
                        pairs += [(T(w3[:, kc, :], wt.cells), src[kg * 16 + kc]) for kc in range(16)]
                    mm_group(ps, pairs)
                    xc = xTc(oc)
                    dve(lambda e, ps=ps, xc=xc: e.tensor_tensor(xc.ap, ps.ap, xc.ap, ALU.add), [ps, xc], [xc])
            if DEBUG:
                mall = A(77056, BF16, [16, 512])
                dma(lambda e: e.dma_start(out=dbg_m.ap().rearrange("(c p) t -> p c t", p=128)[:, :, t0:t0 + 512], in_=mall.ap), [mall], [T(None, DR("dbg_m", s))], "dbg2")
            set_piece(l, P_WOUT)
            proj_residual(mT)
            if DEBUG:
                dma(lambda e: e.dma_start(out=dbg_xc.ap().rearrange("(c p) t -> p c t", p=128)[:, :, t0:t0 + 512], in_=xT.ap), [xT], [T(None, DR("dbg_xc", s))], "dbg3")
            hT = rmsnorm2(DEPTH + l, 0, 16384)
            set_piece(l, P_WQ)
            qxT = [chunk(20480, BF16, 512, c) for c in range(16)]
            oxT = [chunk(36864, BF16, 512, c) for c in range(16)]
            oxall = A(36864, BF16, [16, 512])
            for oc in range(16):
                wt = next_piece(l)
                ps = PS(next_ps())
                w3 = wt.ap.rearrange("p (k n) -> p k n", n=128)
                mm_group(ps, [(T(w3[:, kc, :], wt.cells), hT[kc]) for kc in range(16)])
                evac_copy(qxT[oc], ps, scale=float(512 ** -0.5))
            for tt in range(4):
                for h in range(4):
                    it = (tt * 4 + h) % 2
                    ps = PS(next_ps())
                    psv = T(ps.ap[:, 0:256], ps.cells)
                    mm_group(psv, [(T(qxT[h * 4 + dc].ap[:, tt * 128:(tt + 1) * 128], qxT[h * 4 + dc].cells), T(KmT.ap[:, h * 4 + dc, :], KmT.cells)) for dc in range(4)])
                    st4 = T(A(53248, F32, [16]).ap[:, it * 4:(it + 1) * 4], A(53248, F32, [16]).cells)
                    pp = A(54272 + it * 512, BF16, [256])
                    pT = A(55296 + it * 512, BF16, [2, 128])
                    dve(lambda e, psv=psv, st4=st4: e.reduce_max(st4.ap[:, 0:1], psv.ap, AX.X), [psv], [st4])
                    dve(lambda e, st4=st4: e.tensor_scalar(st4.ap[:, 1:2], st4.ap[:, 0:1], -1.0, None, ALU.mult), [st4], [st4])
                    act(lambda e, psv=psv, pp=pp, st4=st4: e.activation(pp.ap, psv.ap, AF.Exp, bias=st4.ap[:, 1:2]), [psv, st4], [pp])
                    dve(lambda e, pp=pp, st4=st4: e.reduce_sum(st4.ap[:, 2:3], pp.ap, AX.X), [pp], [st4])
                    dve(lambda e, st4=st4: e.reciprocal(st4.ap[:, 3:4], st4.ap[:, 2:3]), [st4], [st4])
                    dve(lambda e, pp=pp, st4=st4: e.tensor_scalar(pp.ap, pp.ap, st4.ap[:, 3:4], None, ALU.mult), [pp, st4], [pp])
                    pb = PB(next_pb())

                    def tfn(e, pb=pb, pp=pp):
                        e.transpose(pb.ap[:, 0:128], pp.ap[:, 0:128], ident.ap)
                        return e.transpose(pb.ap[:, 128:256], pp.ap[:, 128:256], ident.ap)
                    S.add("pe", tfn, [pp, ident], [pb])
                    act(lambda e, pb=pb, pT=pT: e.copy(pT.ap.rearrange("p a b -> p (a b)"), pb.ap[:, 0:256]), [pb], [pT])
                    po = PS(next_ps())

                    def ofn(e, po=po, pT=pT, h=h):
                        ins = None
                        for dc in range(4):
                            for mc in range(2):
                                ins = e.matmul(po.ap[:, dc * 128:(dc + 1) * 128], Vm.ap[:, mc, (h * 4 + dc) * 128:(h * 4 + dc + 1) * 128], pT.ap[:, mc, :],
                                               start=(mc == 0), stop=(mc == 1))
                        return ins
                    S.add("pe", ofn, [Vm, pT], [po])
                    act(lambda e, po=po, h=h, tt=tt: e.copy(oxall.ap[:, h * 4:(h + 1) * 4, tt * 128:(tt + 1) * 128], po.ap.rearrange("p (a b) -> p a b", b=128)),
                        [po], [oxT[h * 4 + dc] for dc in range(4)])
            set_piece(l, P_WO)
            proj_residual(oxT)
            if DEBUG:
                dma(lambda e: e.dma_start(out=dbg_xd.ap().rearrange("(c p) t -> p c t", p=128)[:, :, t0:t0 + 512], in_=xT.ap), [xT], [T(None, DR("dbg_xd", s))], "dbg4")
            fT = rmsnorm2(3 * DEPTH + l, 0, 16384)
            set_piece(l, P_W1)
            hfT = [chunk(24576, BF16, 512, c) for c in range(64)]
            for oc in range(64):
                wt = next_piece(l)
                ps = PS(next_ps())
                w3 = wt.ap.rearrange("p (k n) -> p k n", n=128)
                mm_group(ps, [(T(w3[:, kc, :], wt.cells), fT[kc]) for kc in range(16)])
                rl = A(20480 + (oc % 2) * 2048, F32, [512])
                act(lambda e, ps=ps, rl=rl: e.activation(rl.ap, ps.ap, AF.Relu), [ps], [rl])
                S.add("pool", lambda e, rl=rl, oc=oc: e.tensor_tensor(hfT[oc].ap, rl.ap, rl.ap, ALU.mult), [rl], [hfT[oc]])
            set_piece(l, P_W2)
            proj_residual(hfT, nk_pieces=4)
            if not last:
                dma(lambda e: e.dma_start(out=xT_d.ap().rearrange("(fc p) t -> p fc t", p=128)[:, :, t0:t0 + 512], in_=xT.ap),
                    [xT], [T(None, DR("xT_d", s))], "xst")
            else:
                psf_ = PS(next_ps())
                sqs_f = [A(16384 + k * 1024, BF16, [512]) for k in range(2)]
                rstd_f = A(18432, F32, [512])
                for fc in range(16):
                    sq = sqs_f[fc % 2]
                    act(lambda e, sq=sq, fc=fc: e.activation(sq.ap, xT.ap[:, fc, :], AF.Square), [xTc(fc)], [sq])
                    S.add("pe", lambda e, sq=sq, fc=fc, ps=psf_: e.matmul(ps.ap, ones2048.ap, sq.ap, start=(fc == 0), stop=(fc == 15)),
                          [sq, ones2048] + ([psf_] if fc else []), [psf_])
                rsqrt_eps(rstd_f, psf_)
                for fc in range(16):
                    xc = xTc(fc)
                    dve(lambda e, xc=xc, fc=fc, rstd_f=rstd_f: e.scalar_tensor_tensor(xc.ap, xc.ap, gcols.ap[:, 4 * DEPTH, fc:fc + 1], rstd_f.ap, ALU.mult, ALU.mult),
                        [xc, gcols, rstd_f], [xc])
                dma(lambda e: e.dma_start(out=yT_out.ap().rearrange("(fc p) t -> p fc t", p=128)[:, :, t0:t0 + 512], in_=xT.ap),
                    [xT], [T(None, DR("yT", s))], "xst")

        setup()
        ident_in = din("ident", [128, 128])
        tid = A(0, F32, [128])
        dma(lambda e: e.dma_start(out=tid.ap, in_=ident_in.ap()), [], [tid], "k_tid")
        dve(lambda e: e.tensor_copy(ident.ap, tid.ap), [tid], [ident])
        for l in range(DEPTH):
            layer_consts(l)
            for s in range(NS):
                phase_a(l, s)
            for s in range(NS):
                if s == 0:
                    mem_kv(l, 0)
                elif s * 8 == SEAM:
                    mem_kv(l, 1)
                phase_b(l, s, last=(l == DEPTH - 1))

        sval, semnames = S.emit(nc, None)
        sems = {k: es.enter_context(nc.semaphore(k)) for k in semnames}
        assert len(sems) < 140, len(sems)
        block = es.enter_context(nc.Block())
        engmap = {"pe": "tensor", "act": "scalar", "dve": "vector", "pool": "gpsimd", "sp": "sync"}
        per_eng = {k: [] for k in engmap}
        for i, (eng, fn, deps, semkey) in enumerate(S.ops):
            per_eng[eng].append(i)
        final_waits = {}
        for i, (eng, fn, deps, semkey) in enumerate(S.ops):
            if semkey is not None:
                final_waits[sval[i][0]] = sval[i][1]

        def make_body(eng):
            def body(e):
                waited = {}
                for i in per_eng[eng]:
                    _, fn, deps, semkey = S.ops[i]
                    need = {}
                    for d in deps:
                        sv = sval[d]
                        if sv is None:
                            continue
                        if S.ops[d][0] == "pe" and eng == "pe" and S.ops[d][3] is None:
                            continue
                        if need.get(sv[0], 0) < sv[1]:
                            need[sv[0]] = sv[1]
                    for k, v in need.items():
                        if waited.get(k, 0) < v:
                            e.wait_ge(sems[k], v)
                            waited[k] = v
                    ins = fn(e)
                    if sval[i] is not None:
                        ins.then_inc(sems[sval[i][0]], 16 if (semkey is not None and not semkey.startswith("cc")) else 1)
                if eng == "pool":
                    for k, v in final_waits.items():
                        e.wait_ge(sems[k], v)
            return body
        block.tensor(make_body("pe"))
        block.scalar(make_body("act"))
        block.vector(make_body("dve"))
        block.gpsimd(make_body("pool"))
        block.sync(make_body("sp"))
    return nc


def _typeA(Wm, oc, k0=0):
    blk = Wm[k0 * 128:(k0 + 16) * 128, oc * 128:(oc + 1) * 128]
    return blk.reshape(16, 128, 128).transpose(1, 0, 2).reshape(128, 2048)


def _typeB(Wm, c0, kg):
    blk = Wm[kg * 512:(kg + 1) * 512, c0:c0 + 512]
    return blk.reshape(4, 128, 512).transpose(1, 0, 2).reshape(128, 2048)


def _typeBr(Wm, j):
    blk = Wm[:, j * 256:(j + 1) * 256]
    return blk.reshape(8, 128, 2, 128).transpose(1, 2, 0, 3).reshape(128, 2048)


def _layer_blob(w_in, wbc, wbs, wbn, w_out, wq, wk, wv, wo, w1, w2):
    ps = []
    for c in range(8):
        ps.append(_typeA(w_in, 16 + c))
    for cb in range(2):
        for kg in range(4):
            ps.append(_typeB(w_in, 3072 + cb * 512, kg))
    for c in range(16):
        ps.append(_typeA(w_in, c))
    for c in range(16):
        ps.append(_typeA(w_in, 32 + c))
    for cb in range(2):
        for kg in range(4):
            ps.append(_typeB(w_in, 6144 + cb * 512, kg))
    for c in range(48):
        ps.append(_typeA(w_in, 56 + c))
    for j in range(8):
        ps.append(_typeBr(wbc, j)); ps.append(_typeBr(wbs, j)); ps.append(_typeBr(wbn, j))
    for Wm in (w_out, wq, wk):
        for oc in range(16):
            ps.append(_typeA(Wm, oc))
    for cb in range(4):
        for kg in range(4):
            ps.append(_typeB(wv, cb * 512, kg))
    for oc in range(16):
        ps.append(_typeA(wo, oc))
    for oc in range(64):
        ps.append(_typeA(w1, oc))
    for oc in range(16):
        for kg in range(4):
            ps.append(_typeA(w2, oc, k0=kg * 16))
    assert len(ps) == PIECES_PER_LAYER
    return np.concatenate(ps, axis=0)


def _na_tables():
    cols = np.arange(64)
    c_start = np.clip(cols - 8, 0, 48)
    kc = np.arange(64)
    inwin = (kc[None, :] >= c_start[:, None]) & (kc[None, :] < c_start[:, None] + 16)
    dc = np.clip(kc[None, :] - cols[:, None] + 15, 0, 30)
    return inwin, dc


def host_prep(inputs, W, DEPTH, SEAM, seqs, NCB=NCORES):
    f32 = np.float32
    L = DEPTH
    inwin, dc = _na_tables()
    g = lambda k: np.asarray(inputs[k], f32)
    blobs = []
    for l in range(L):
        blobs.append(_layer_blob(g("w_in")[l], g("w_br_conv")[l], g("w_br_sgu")[l], g("w_br_na")[l], g("w_out")[l],
                                 g("xa_wq")[l], g("xa_wk")[l], g("xa_wv")[l], g("xa_wo")[l], g("ffn_w1")[l], g("ffn_w2")[l]))
    ppc = PIECES_PER_LAYER // NCB
    wsh = [np.concatenate([b[c * ppc * 128:(c + 1) * ppc * 128] for b in blobs], axis=0) for c in range(NCB)]
    del blobs
    gl = np.concatenate([g("norm_mix"), g("norm_xa"), g("norm_mem"), g("norm_ffn"), g("norm_final")[None]], axis=0)
    gcols = np.ascontiguousarray(gl.reshape(4 * L + 1, 16, 128).transpose(2, 0, 1)).reshape(128, -1)
    convp = np.zeros((L, 128, 8, 34), f32)
    convp[:, :, :, 0:31] = g("conv_dw").transpose(0, 2, 1).reshape(L, 8, 128, 31).transpose(0, 2, 1, 3)
    for i, k in enumerate(("conv_b", "conv_ln_g", "conv_ln_b")):
        convp[:, :, :, 31 + i] = g(k).reshape(L, 8, 128).transpose(0, 2, 1)
    convp = convp.reshape(L, 128, 8 * 34)
    sgubc = np.stack([g("sgu_ln_g"), g("sgu_ln_b"), g("sgu_b").reshape(L, 1024)], axis=1)
    sguw = np.ascontiguousarray(g("sgu_w").transpose(0, 3, 1, 2)).reshape(L, 128, 1024)
    rpb = g("na_rpb")
    gath = rpb[:, :, :, dc]
    gath = gath * inwin[None, None, None]
    gath = gath.reshape(L, 8, 2, 15, 64, 64).transpose(0, 2, 4, 1, 3, 5).reshape(L, 128, 8 * 15 * 64)
    cm = np.where(inwin, 0.0, NEG).astype(f32)
    cmask = np.broadcast_to(cm[None, :, None, None, :], (2, 64, 8, 15, 64)).reshape(128, 8 * 15 * 64)
    ident = np.eye(128, dtype=f32)
    in_maps = []
    for c in range(NCB):
        sq = seqs[c]
        xT = np.zeros((D, W * 64), f32)
        memT = np.zeros((2, D, NMEM), f32)
        seam = np.zeros((8, 16), f32)
        flag = np.ones((128, 1), f32)
        for ri in range(8):
            r = SEAM - 4 + ri
            for j in range(16):
                kr = SEAM - 8 + j
                if sq is not None and len(sq) == 2:
                    ok = (kr < SEAM) if r < SEAM else (kr >= SEAM)
                    if r < SEAM:
                        ok = SEAM - 8 <= kr < SEAM
                    else:
                        ok = SEAM <= kr < SEAM + 8
                else:
                    ok = r - 4 <= kr < r + 4
                seam[ri, j] = 0.0 if ok else NEG
        if sq is not None:
            if len(sq) == 2:
                flag[:] = 0.0
                xT[:, :SEAM * 64] = sq[0][0].T
                xT[:, SEAM * 64:] = sq[1][0].T
                memT[0] = sq[0][1].T
                memT[1] = sq[1][1].T
            else:
                xT[:] = sq[0][0].T
                memT[0] = sq[0][1].T
                memT[1] = sq[0][1].T
        in_maps.append({
            "xT": xT, "memT": memT, "wsh": wsh[c], "gcols": gcols.astype(f32), "convp": convp, "sgubc": sgubc.astype(f32),
            "sguw": sguw.astype(f32), "rpbg": np.ascontiguousarray(gath, dtype=f32), "cmask": np.ascontiguousarray(cmask, dtype=f32),
            "seam": np.ascontiguousarray(np.repeat(seam, 64, axis=1)), "flag": flag, "ident": ident,
        })
    return in_maps


_NC_CACHE = {}


def run(inputs, W, DEPTH, SEAM, seqs):
    key = (W, DEPTH, SEAM)
    if key not in _NC_CACHE:
        _NC_CACHE[key] = build(W, DEPTH, SEAM)
    nc = _NC_CACHE[key]
    in_maps = host_prep(inputs, W, DEPTH, SEAM, seqs)
    res = run_bass_kernel_spmd(nc, in_maps, core_ids=list(range(NCORES)))
    run.last = res.results
    return [r["yT"] for r in res.results]


def kernel(**inputs):
    xp = np.asarray(inputs["x_prompt"], np.float32)
    xs = np.asarray(inputs["x_sample"], np.float32)
    mp = np.asarray(inputs["mem_prompt"], np.float32)
    ms = np.asarray(inputs["mem_sample"], np.float32)
    seqs = [[(xs[i], ms[i])] for i in range(4)] + [[(xp[0], mp[0]), (xp[1], mp[1])], None, None, None]
    outs = run(inputs, 128, 4, 64, seqs)
    y_sample = np.stack([np.ascontiguousarray(outs[i].T) for i in range(4)], axis=0)
    yp = outs[4].T
    y_prompt = np.stack([np.ascontiguousarray(yp[:4096]), np.ascontiguousarray(yp[4096:])], axis=0)
    return (y_prompt.astype(np.float32), y_sample.astype(np.float32))
```
